# Optimizing a Trainium2 kernel written in Bass

```python
import jax, jax.numpy as jnp
from jax import lax
import numpy as np

D_MODEL = 1024
BATCH = 4
SEQ = 8192
DEPTH = 2

HEAD_DIM = 64
ROPE_THETA = 10000.0
GRID_W = 64
EPS = 1e-6
NEG_INF = -1e30

A_HEADS = 8
A_KV_HEADS = 2
A_BLOCK = 128

LRU_WIDTH = 512
LRU_BLOCKS = 8
LRU_BLOCK_W = LRU_WIDTH // LRU_BLOCKS
LRU_C = 8.0
CONV_W = 4
CONV_LEFT = 2

C_HEADS = 8
C_KV_HEADS = 2
C_HALF_WINDOW = 128
C_BLOCK = 128

D_PATTERNS = ((128, 1), (512, 4), (2048, 16))
D_GROUPS = 3
D_HEADS = 4
D_BLOCK = 64

N_BRANCH = 4
MLP_HIDDEN = 4 * D_MODEL

IN_SIZES = (
    A_HEADS * HEAD_DIM, A_KV_HEADS * HEAD_DIM, A_KV_HEADS * HEAD_DIM,
    LRU_WIDTH, LRU_WIDTH,
    C_HEADS * HEAD_DIM, C_KV_HEADS * HEAD_DIM, C_KV_HEADS * HEAD_DIM,
    D_GROUPS * D_HEADS * HEAD_DIM, D_GROUPS * D_HEADS * HEAD_DIM, D_GROUPS * D_HEADS * HEAD_DIM,
    N_BRANCH * D_MODEL,
)
N_IN = sum(IN_SIZES)

kernel_name = "hybrid_parallel_gated_encoder"


def rmsnorm(x, g):
    xf = x.astype(jnp.float32)
    y = xf * lax.rsqrt(jnp.mean(xf * xf, axis=-1, keepdims=True) + EPS) * g.astype(jnp.float32)
    return y.astype(x.dtype)


def rope_tables(pos, dim):
    inv = ROPE_THETA ** (-jnp.arange(0, dim, 2, dtype=jnp.float32) / dim)
    ang = pos.astype(jnp.float32)[:, None] * inv[None, :]
    return jnp.cos(ang), jnp.sin(ang)


def apply_rotary(x, cos, sin):
    half = x.shape[-1] // 2
    xf = x.astype(jnp.float32)
    x1, x2 = xf[..., :half], xf[..., half:]
    c, s = cos[:, None, :], sin[:, None, :]
    return jnp.concatenate([x1 * c - x2 * s, x2 * c + x1 * s], axis=-1).astype(x.dtype)


def axial_rotary(x, row_cs, col_cs):
    h = x.shape[-1] // 2
    return jnp.concatenate([apply_rotary(x[..., :h], *row_cs), apply_rotary(x[..., h:], *col_cs)], axis=-1)


def dense_block_attention(q, k, v, block):
    b, s, hk, g, hd = q.shape
    nb = s // block
    qb = jnp.moveaxis(q.reshape(b, nb, block, hk, g, hd), 1, 0)

    def one_block(qblk):
        sc = jnp.einsum('bqhgd,bkhd->bhgqk', qblk, k).astype(jnp.float32) * (hd ** -0.5)
        p = jax.nn.softmax(sc, axis=-1)
        return jnp.einsum('bhgqk,bkhd->bqhgd', p.astype(v.dtype), v)

    o = lax.map(one_block, qb)
    return jnp.moveaxis(o, 0, 1).reshape(b, s, hk * g * hd)


def banded_attention(q, k, v, half_window, block, sink=None):
    b, L, hk, g, hd = q.shape
    nb = -(-L // block)
    lp = nb * block
    nw = -(-half_window // block)
    q = jnp.pad(q, ((0, 0), (0, lp - L), (0, 0), (0, 0), (0, 0)))
    pad_k = ((0, 0), (nw * block, lp - L + nw * block), (0, 0), (0, 0))
    kb = jnp.pad(k, pad_k).reshape(b, nb + 2 * nw, block, hk, hd)
    vb = jnp.pad(v, pad_k).reshape(b, nb + 2 * nw, block, hk, hd)
    kw = jnp.concatenate([kb[:, j:j + nb] for j in range(2 * nw + 1)], axis=2)
    vw = jnp.concatenate([vb[:, j:j + nb] for j in range(2 * nw + 1)], axis=2)
    qb = q.reshape(b, nb, block, hk, g, hd)
    sc = jnp.einsum('bnqhgd,bnkhd->bnhgqk', qb, kw).astype(jnp.float32) * (hd ** -0.5)
    qpos = jnp.arange(nb)[:, None] * block + jnp.arange(block)[None, :]
    kpos = (jnp.arange(nb)[:, None] - nw) * block + jnp.arange((2 * nw + 1) * block)[None, :]
    rel = kpos[:, None, :] - qpos[:, :, None]
    valid = (jnp.abs(rel) <= half_window) & (kpos[:, None, :] >= 0) & (kpos[:, None, :] < L)
    sc = jnp.where(valid[None, :, None, None], sc, NEG_INF)
    m = jnp.max(sc, axis=-1)
    if sink is not None:
        sink_b = sink.astype(jnp.float32).reshape(hk, g)[None, None, :, :, None]
        m = jnp.maximum(m, sink_b)
    p = jnp.exp(sc - m[..., None])
    l = jnp.sum(p, axis=-1)
    if sink is not None:
        l = l + jnp.exp(sink_b - m)
    o = jnp.einsum('bnhgqk,bnkhd->bnqhgd', p.astype(v.dtype), vw).astype(jnp.float32)
    l_t = jnp.transpose(l, (0, 1, 4, 2, 3))
    lse = jnp.transpose(m, (0, 1, 4, 2, 3)) + jnp.log(l_t)
    o = (o / l_t[..., None]).reshape(b, lp, hk, g, hd)[:, :L].astype(v.dtype)
    return o, lse.reshape(b, lp, hk, g)[:, :L]


def mixer_a(q, k, v, qk_g, row_cs, col_cs):
    b, s = q.shape[:2]
    q = rmsnorm(q.reshape(b, s, A_HEADS, HEAD_DIM), qk_g[0])
    k = rmsnorm(k.reshape(b, s, A_KV_HEADS, HEAD_DIM), qk_g[1])
    v = v.reshape(b, s, A_KV_HEADS, HEAD_DIM)
    q = axial_rotary(q, row_cs, col_cs).reshape(b, s, A_KV_HEADS, A_HEADS // A_KV_HEADS, HEAD_DIM)
    k = axial_rotary(k, row_cs, col_cs)
    return dense_block_attention(q, k, v, A_BLOCK)


def rg_lru_scan(xc, gate_w, gate_b, lam, reverse):
    b, s, w = xc.shape
    xh = xc.reshape(b, s, LRU_BLOCKS, LRU_BLOCK_W)
    gl = jnp.einsum('bsnc,gncd->gbsnd', xh, gate_w.astype(jnp.float32)).reshape(2, b, s, w)
    gl = gl + gate_b.astype(jnp.float32)[:, None, None, :]
    r = jax.nn.sigmoid(gl[0])
    i = jax.nn.sigmoid(gl[1])
    log_a = -LRU_C * r * jax.nn.softplus(-lam.astype(jnp.float32))
    a = jnp.exp(log_a)
    u = jnp.sqrt(-jnp.expm1(2.0 * log_a)) * (i * xc)

    def combine(e1, e2):
        a1, b1 = e1
        a2, b2 = e2
        return a1 * a2, a2 * b1 + b2

    _, h = lax.associative_scan(combine, (a, u), axis=1, reverse=reverse)
    return h


def mixer_b(xb, yb, conv_w, conv_b, gate_w, gate_b, lam):
    s = xb.shape[1]
    xf = xb.astype(jnp.float32)
    xp = jnp.pad(xf, ((0, 0), (CONV_LEFT, CONV_W - 1 - CONV_LEFT), (0, 0)))
    xc = sum(xp[:, j:j + s] * conv_w[j].astype(jnp.float32) for j in range(CONV_W)) + conv_b.astype(jnp.float32)
    h = rg_lru_scan(xc, gate_w[0], gate_b[0], lam[0], False) + rg_lru_scan(xc, gate_w[1], gate_b[1], lam[1], True)
    return (h * jax.nn.gelu(yb.astype(jnp.float32))).astype(xb.dtype)


def mixer_c(q, k, v, sink, cs):
    b, s = q.shape[:2]
    q = apply_rotary(q.reshape(b, s, C_HEADS, HEAD_DIM), *cs).reshape(b, s, C_KV_HEADS, C_HEADS // C_KV_HEADS, HEAD_DIM)
    k = apply_rotary(k.reshape(b, s, C_KV_HEADS, HEAD_DIM), *cs)
    v = v.reshape(b, s, C_KV_HEADS, HEAD_DIM)
    o, _ = banded_attention(q, k, v, C_HALF_WINDOW, C_BLOCK, sink)
    return o.reshape(b, s, C_HEADS * HEAD_DIM)


def mixer_d(q, k, v, cs):
    b, s = q.shape[:2]
    n_h = D_GROUPS * D_HEADS
    q = apply_rotary(q.reshape(b, s, n_h, HEAD_DIM), *cs).reshape(b, s, D_GROUPS, D_HEADS, HEAD_DIM)
    k = apply_rotary(k.reshape(b, s, n_h, HEAD_DIM), *cs).reshape(b, s, D_GROUPS, D_HEADS, HEAD_DIM)
    v = v.reshape(b, s, D_GROUPS, D_HEADS, HEAD_DIM)
    outs, lses = [], []
    for gi, (window, dil) in enumerate(D_PATTERNS):
        L = s // dil

        def to_residue(t):
            t = t[:, :, gi].reshape(b, L, dil, D_HEADS, HEAD_DIM)
            return jnp.transpose(t, (0, 2, 1, 3, 4)).reshape(b * dil, L, D_HEADS, HEAD_DIM)

        o, lse = banded_attention(to_residue(q)[:, :, :, None, :], to_residue(k), to_residue(v),
                                  window // (2 * dil), D_BLOCK)
        o = jnp.transpose(o.reshape(b, dil, L, D_HEADS, HEAD_DIM), (0, 2, 1, 3, 4)).reshape(b, s, D_HEADS, HEAD_DIM)
        lse = jnp.transpose(lse.reshape(b, dil, L, D_HEADS), (0, 2, 1, 3)).reshape(b, s, D_HEADS)
        outs.append(o.astype(jnp.float32))
        lses.append(lse)
    wts = jax.nn.softmax(jnp.stack(lses, axis=0), axis=0)
    out = jnp.sum(wts[..., None] * jnp.stack(outs, axis=0), axis=0)
    return out.reshape(b, s, D_HEADS * HEAD_DIM).astype(q.dtype)


def setup_inputs(seed: int = 0) -> dict:
    key = jax.random.key(seed)
    ks = jax.random.split(key, 20)
    f32 = jnp.float32

    def nrm(k, shape, scale):
        return jax.random.normal(k, shape, f32) * scale

    a0 = jax.random.uniform(ks[9], (DEPTH, 2, LRU_WIDTH), f32, 0.9, 0.999)
    p = a0 ** (1.0 / LRU_C)
    lru_lambda = jnp.log(p) - jnp.log1p(-p)
    return {
        "x": nrm(ks[0], (BATCH, SEQ, D_MODEL), 1.0),
        "norm_mix_g": 1.0 + nrm(ks[1], (DEPTH, D_MODEL), 0.02),
        "w_in": nrm(ks[2], (DEPTH, D_MODEL, N_IN), D_MODEL ** -0.5),
        "gate_bias": nrm(ks[3], (DEPTH, N_BRANCH, D_MODEL), 0.02),
        "qk_norm_g": 1.0 + nrm(ks[4], (DEPTH, 2, HEAD_DIM), 0.02),
        "conv_w": nrm(ks[5], (DEPTH, CONV_W, LRU_WIDTH), CONV_W ** -0.5),
        "conv_b": nrm(ks[6], (DEPTH, LRU_WIDTH), 0.02),
        "lru_gate_w": nrm(ks[7], (DEPTH, 2, 2, LRU_BLOCKS, LRU_BLOCK_W, LRU_BLOCK_W), LRU_BLOCK_W ** -0.5),
        "lru_gate_b": nrm(ks[8], (DEPTH, 2, 2, LRU_WIDTH), 0.02),
        "lru_lambda": lru_lambda,
        "sink_logit": nrm(ks[10], (DEPTH, C_HEADS), 1.0),
        "w_proj_a": nrm(ks[11], (DEPTH, A_HEADS * HEAD_DIM, D_MODEL), (A_HEADS * HEAD_DIM) ** -0.5),
        "w_proj_b": nrm(ks[12], (DEPTH, LRU_WIDTH, D_MODEL), LRU_WIDTH ** -0.5),
        "w_proj_c": nrm(ks[13], (DEPTH, C_HEADS * HEAD_DIM, D_MODEL), (C_HEADS * HEAD_DIM) ** -0.5),
        "w_proj_d": nrm(ks[14], (DEPTH, D_HEADS * HEAD_DIM, D_MODEL), (D_HEADS * HEAD_DIM) ** -0.5),
        "w_out": nrm(ks[15], (DEPTH, D_MODEL, D_MODEL), D_MODEL ** -0.5),
        "norm_mlp_g": 1.0 + nrm(ks[16], (DEPTH, D_MODEL), 0.02),
        "w_mlp1": nrm(ks[17], (DEPTH, D_MODEL, MLP_HIDDEN), D_MODEL ** -0.5),
        "w_mlp2": nrm(ks[18], (DEPTH, MLP_HIDDEN, D_MODEL), MLP_HIDDEN ** -0.5),
        "norm_final_g": 1.0 + nrm(ks[19], (D_MODEL,), 0.02),
    }


def reference(x, norm_mix_g, w_in, gate_bias, qk_norm_g, conv_w, conv_b, lru_gate_w, lru_gate_b,
              lru_lambda, sink_logit, w_proj_a, w_proj_b, w_proj_c, w_proj_d, w_out, norm_mlp_g,
              w_mlp1, w_mlp2, norm_final_g):
    b, s, _ = x.shape
    rows = s // GRID_W
    pos = jnp.arange(s)
    row_pos = jnp.repeat(jnp.arange(rows), GRID_W)
    col_pos = jnp.tile(jnp.arange(GRID_W), rows)
    row_cs = rope_tables(row_pos, HEAD_DIM // 2)
    col_cs = rope_tables(col_pos, HEAD_DIM // 2)
    seq_cs = rope_tables(pos, HEAD_DIM)
    split_points = np.cumsum(IN_SIZES)[:-1]

    for l in range(DEPTH):
        hn = rmsnorm(x, norm_mix_g[l])
        proj = hn @ w_in[l]
        (aq, ak, av, bx, by, cq, ck, cv, dq, dk, dv, gl) = jnp.split(proj, split_points, axis=-1)
        ya = mixer_a(aq, ak, av, qk_norm_g[l], row_cs, col_cs)
        yb = mixer_b(bx, by, conv_w[l], conv_b[l], lru_gate_w[l], lru_gate_b[l], lru_lambda[l])
        yc = mixer_c(cq, ck, cv, sink_logit[l], seq_cs)
        yd = mixer_d(dq, dk, dv, seq_cs)
        gates = jax.nn.sigmoid((gl.reshape(b, s, N_BRANCH, D_MODEL) + gate_bias[l]).astype(jnp.float32)).astype(x.dtype)
        merged = (gates[:, :, 0] * (ya @ w_proj_a[l]) + gates[:, :, 1] * (yb @ w_proj_b[l])
                  + gates[:, :, 2] * (yc @ w_proj_c[l]) + gates[:, :, 3] * (yd @ w_proj_d[l]))
        x = x + merged @ w_out[l]
        hn = rmsnorm(x, norm_mlp_g[l])
        x = x + jnp.square(jax.nn.relu(hn @ w_mlp1[l])) @ w_mlp2[l]
    return rmsnorm(x, norm_final_g)
```

```python
import math
from contextlib import ExitStack
import numpy as np
import concourse.bass as bass
import concourse.mybir as mybir
from concourse.bass_utils import run_bass_kernel_spmd

F32 = mybir.dt.float32
BF16 = mybir.dt.bfloat16
AF = mybir.ActivationFunctionType
ALU = mybir.AluOpType
AX = mybir.AxisListType

D_MODEL = 1024
N_IN = 8960
EPS = 1e-6
THETA = 10000.0
GRID_W = 64
O_AQ, O_AK, O_AV, O_BX, O_BY, O_CQ, O_CK, O_CV, O_DQ, O_DK, O_DV, O_GL = (
    0, 512, 640, 768, 1280, 1792, 2304, 2432, 2560, 3328, 4096, 4864)
NDS = 8


class Buf:
    __slots__ = ("name", "w", "r", "psum")

    def __init__(self, name="", psum=False):
        self.name = name
        self.w = None
        self.r = {}
        self.psum = psum


class Tl:
    def __init__(self, t, name=""):
        self.t = t
        self.b = Buf(name)

    def __getitem__(self, k):
        return self.t[k]


class Op:
    __slots__ = ("eng", "fn", "deps", "sig", "need", "dma", "phase", "prev", "uid")


class Prog:
    ENG = ("pe", "act", "dve", "pool", "sp")

    def __init__(self, nc, st):
        self.nc = nc
        self.sem = {e: st.enter_context(nc.semaphore("s_" + e)) for e in self.ENG}
        self.dsem = {q: [st.enter_context(nc.semaphore("d_%s%d" % (q, i))) for i in range(NDS)] for q in ("sp", "pool")}
        self.cnt = {e: 0 for e in self.ENG}
        self.dcnt = {q: [0] * NDS for q in ("sp", "pool")}
        self.dnext = {q: 0 for q in ("sp", "pool")}
        self.dlast = {q: [None] * NDS for q in ("sp", "pool")}
        self.waited = {e: {} for e in self.ENG}
        self.ops = []
        self.phase = 0
        self.uid = 0
        self.semkey = {}
        for e in self.ENG:
            self.semkey[("c", e)] = self.sem[e]
        for q in ("sp", "pool"):
            for i in range(NDS):
                self.semkey[("d", q, i)] = self.dsem[q][i]

    def op(self, eng, fn, reads=(), writes=(), dma=0):
        o = Op()
        o.eng, o.fn, o.dma, o.phase, o.need, o.sig, o.prev = eng, fn, dma, self.phase, False, None, None
        o.uid = self.uid
        self.uid += 1
        deps = []
        for b in reads:
            b = b.b if isinstance(b, Tl) else b
            if b.w is not None:
                deps.append((b.w, "raw"))
            if b.psum:
                for r in b.r.values():
                    deps.append((r, "rar"))
        for b in writes:
            b = b.b if isinstance(b, Tl) else b
            if b.w is not None:
                deps.append((b.w, "waw"))
            for r in b.r.values():
                deps.append((r, "war"))
        o.deps = []
        for d, k in deps:
            if d.phase != self.phase or d is o:
                continue
            if d.eng == eng and not d.dma and not dma:
                if eng == "pe" or k in ("war", "rar"):
                    continue
            d.need = True
            o.deps.append(d)
        for b in reads:
            b = b.b if isinstance(b, Tl) else b
            key = eng if not dma else ("dma", o.uid)
            b.r[key] = o
        for b in writes:
            b = b.b if isinstance(b, Tl) else b
            b.w = o
            b.r = {}
        self.ops.append(o)
        return o

    def dma(self, pairs, reads=(), writes=(), q="sp"):
        def fn(e, sem, pairs=pairs):
            for (o_, i_) in pairs:
                e.dma_start(out=o_, in_=i_).then_inc(sem, 16)
        return self.op(q, fn, reads, writes, dma=len(pairs))

    def emit(self):
        nc = self.nc
        ops = self.ops
        for o in ops:
            if o.dma:
                q = o.eng
                i = self.dnext[q]
                self.dnext[q] = (i + 1) % NDS
                o.prev = self.dlast[q][i]
                self.dcnt[q][i] += 16 * o.dma
                o.sig = (("d", q, i), self.dcnt[q][i])
                self.dlast[q][i] = o
            elif o.need:
                self.cnt[o.eng] += 1
                o.sig = (("c", o.eng), self.cnt[o.eng])
        last = {}
        for o in ops:
            if not o.dma:
                last[o.eng] = o
        for e, o in last.items():
            if o.sig is None:
                self.cnt[e] += 1
                o.sig = (("c", e), self.cnt[e])
                o.need = True
        finals = [(("c", e), self.cnt[e]) for e in self.ENG]
        for q in ("sp", "pool"):
            for i in range(NDS):
                finals.append((("d", q, i), self.dcnt[q][i]))
        by_eng = {e: [o for o in ops if o.eng == e] for e in self.ENG}
        semkey = self.semkey
        waited = self.waited

        def run(e, name):
            wd = waited[name]
            for o in by_eng[name]:
                waits = {}
                for d in o.deps:
                    s, v = d.sig
                    if waits.get(s, 0) < v:
                        waits[s] = v
                if o.dma and o.prev is not None:
                    s, v = o.prev.sig
                    if waits.get(s, 0) < v:
                        waits[s] = v
                for s, v in waits.items():
                    if wd.get(s, 0) < v:
                        e.wait_ge(semkey[s], v)
                        wd[s] = v
                if o.dma:
                    o.fn(e, semkey[o.sig[0]])
                else:
                    ins = o.fn(e)
                    if o.need:
                        ins.then_inc(semkey[o.sig[0]], 1)
            for s, v in finals:
                if v > 0 and wd.get(s, 0) < v:
                    e.wait_ge(semkey[s], v)
                    wd[s] = v

        with nc.Block() as block:
            @block.tensor
            def _(e):
                run(e, "pe")

            @block.scalar
            def _(e):
                run(e, "act")

            @block.vector
            def _(e):
                run(e, "dve")

            @block.gpsimd
            def _(e):
                run(e, "pool")

            @block.sync
            def _(e):
                run(e, "sp")
        self.ops = []
        self.phase += 1


class Ring:
    def __init__(self, items):
        self.items = items
        self.i = 0

    def next(self):
        t = self.items[self.i % len(self.items)]
        self.i += 1
        return t


def dap(t, offset, ap):
    return bass.AP(t, offset, ap)


def mm(P, out_tl, out_ap, pairs, reads, start=True, stop=True):
    def fn(e):
        n = len(pairs)
        ins = None
        for i, (l, r) in enumerate(pairs):
            ins = e.matmul(out_ap, l, r, start=(start and i == 0), stop=(stop and i == n - 1))
        return ins
    rd = list(reads)
    if not start:
        rd.append(out_tl)
    return P.op("pe", fn, reads=rd, writes=[out_tl])


def act(P, out_tl, out_ap, in_ap, func, reads, bias=None, scale=None, accum=None, extra_w=()):
    def fn(e):
        kw = {}
        if bias is not None:
            kw["bias"] = bias
        if scale is not None:
            kw["scale"] = scale
        if accum is not None:
            kw["accum_out"] = accum
        return e.activation(out=out_ap, in_=in_ap, func=func, **kw)
    return P.op("act", fn, reads=reads, writes=[out_tl] + list(extra_w))


def vop(P, eng, out_tl, reads, f, extra_w=()):
    return P.op(eng, f, reads=reads, writes=[out_tl] + list(extra_w))


class Ctx:
    pass


def build(S, layers, n_layers_total, final_norm=True, debug=False, stop_after=99):
    nc = bass.Bass("TRN2", target_bir_lowering=False)
    NT = S // 128
    NB = S // 512
    LT = n_layers_total

    def din(name, shape, dt=F32):
        return nc.dram_tensor(name, list(shape), dt, kind="ExternalInput")

    I = {}
    I["x"] = din("x", [S, 1024])
    I["norm_mix_g"] = din("norm_mix_g", [LT, 1024])
    I["w_in"] = din("w_in", [LT, 1024, N_IN])
    I["gate_bias"] = din("gate_bias", [LT, 4, 1024])
    I["qk_norm_g"] = din("qk_norm_g", [LT, 2, 64])
    I["conv_w"] = din("conv_w", [LT, 4, 512])
    I["conv_b"] = din("conv_b", [LT, 512])
    I["lru_gate_w"] = din("lru_gate_w", [LT, 2, 2, 8, 64, 64])
    I["lru_gate_b"] = din("lru_gate_b", [LT, 2, 2, 512])
    I["lru_lambda"] = din("lru_lambda", [LT, 2, 512])
    I["sink_logit"] = din("sink_logit", [LT, 8])
    I["w_proj_a"] = din("w_proj_a", [LT, 512, 1024])
    I["w_proj_b"] = din("w_proj_b", [LT, 512, 1024])
    I["w_proj_c"] = din("w_proj_c", [LT, 512, 1024])
    I["w_proj_d"] = din("w_proj_d", [LT, 256, 1024])
    I["w_out"] = din("w_out", [LT, 1024, 1024])
    I["norm_mlp_g"] = din("norm_mlp_g", [LT, 1024])
    I["w_mlp1"] = din("w_mlp1", [LT, 1024, 4096])
    I["w_mlp2"] = din("w_mlp2", [LT, 4096, 1024])
    I["norm_final_g"] = din("norm_final_g", [1024])
    I["fin"] = din("fin", [128, 1])
    I["c_ident"] = din("c_ident", [128, 128])
    I["c_bones"] = din("c_bones", [128, 128])
    I["c_pswA"] = din("c_pswA", [128, 128])
    I["c_pswS"] = din("c_pswS", [128, 128])
    I["c_cosA"] = din("c_cosA", [128, S])
    I["c_sinA"] = din("c_sinA", [128, S])
    I["c_cosS"] = din("c_cosS", [128, S])
    I["c_sinS"] = din("c_sinS", [128, S])
    MASKW = {"C": 1152, "D0": 1152, "D1": 1408, "D2": 2944}
    for k, wd in MASKW.items():
        I["c_mask" + k] = din("c_mask" + k, [128, wd])

    y = nc.dram_tensor("y", [S, 1024], F32, kind="ExternalOutput")

    skind = "ExternalOutput" if debug else "Internal"

    def dscr(name, shape, dt):
        return Tl(nc.dram_tensor(name, list(shape), dt, kind=skind), name)

    R = {}
    R["XNT"] = dscr("s_xnt", [1024, S], BF16)
    R["QA"] = dscr("s_qa", [512, S], BF16)
    R["KA"] = dscr("s_ka", [256, S], BF16)
    R["QC"] = dscr("s_qc", [512, S], BF16)
    R["KC"] = dscr("s_kc", [256, S], BF16)
    R["QD"] = dscr("s_qd", [768, S], BF16)
    R["KD"] = dscr("s_kd", [768, S], BF16)
    R["BXT"] = dscr("s_bxt", [512, S], F32)
    R["BYT"] = dscr("s_byt", [512, S], F32)
    R["VA"] = dscr("s_va", [128, NT * 130], BF16)
    R["VC"] = dscr("s_vc", [128, NT * 130], BF16)
    for gi in range(3):
        R["VD%d" % gi] = dscr("s_vd%d" % gi, [128, NT * 260], BF16)
    R["HF"] = dscr("s_hf", [512, S], F32)
    R["YBT"] = dscr("s_ybt", [512, S], BF16)
    R["YA"] = dscr("s_ya", [64, 8 * S], BF16)
    R["YC"] = dscr("s_yc", [64, 8 * S], BF16)
    R["YD"] = dscr("s_yd", [64, 4 * S], BF16)
    R["X1"] = dscr("s_x1", [S, 1024], F32)
    R["X2"] = dscr("s_x2", [S, 1024], F32)
    SB = {}

    def sbuf_tok(name, i):
        k = (name, i)
        if k not in SB:
            SB[k] = Buf("%s_%s" % (name, i))
        return SB[k]

    with ExitStack() as gst:
        P = Prog(nc, gst)

        def gsb(name, shape, dt):
            return Tl(gst.enter_context(nc.sbuf_tensor(name, list(shape), dt)), name)

        PS = [Tl(gst.enter_context(nc.psum_tensor("ps%d" % i, [128, 512], F32)), "ps%d" % i) for i in range(7)]
        PST = Tl(gst.enter_context(nc.psum_tensor("pst", [128, 1024], BF16)), "pst")
        for t_ in PS + [PST]:
            t_.b.psum = True

        ident_f = gsb("ident_f", [128, 128], F32)
        ident_b = gsb("ident_b", [128, 128], BF16)
        bones_b = gsb("bones_b", [128, 128], BF16)
        pswA_b = gsb("pswA_b", [128, 128], BF16)
        pswS_b = gsb("pswS_b", [128, 128], BF16)
        ones_f = gsb("ones_f", [128, 128], F32)
        cst = gsb("cst", [128, 128], F32)

        P.dma([(ident_f[:], I["c_ident"].ap())], writes=[ident_f])
        vop(P, "dve", ident_b, [ident_f], lambda e: e.tensor_copy(out=ident_b[:], in_=ident_f[:]))
        vop(P, "dve", ones_f, [], lambda e: e.memset(ones_f[:], 1.0))
        for nm, dst in (("c_bones", bones_b), ("c_pswA", pswA_b), ("c_pswS", pswS_b)):
            P.dma([(cst[:], I[nm].ap())], writes=[cst])
            vop(P, "dve", dst, [cst], lambda e, dst=dst: e.tensor_copy(out=dst[:], in_=cst[:]))
        P.emit()

        G = Ctx()
        G.nc, G.P, G.I, G.R, G.PS, G.PST, G.S, G.NT, G.NB = nc, P, I, R, PS, PST, S, NT, NB
        G.ident_f, G.ident_b, G.bones_b, G.pswA_b, G.pswS_b, G.ones_f = ident_f, ident_b, bones_b, pswA_b, pswS_b, ones_f
        G.tok = sbuf_tok
        G.MASKW = MASKW
        G.y = y
        G.debug = debug

        xin = Tl(I["x"], "x_in")
        for li, L in enumerate(layers):
            last = (li == len(layers) - 1)
            x_src = xin if li == 0 else R["X2"]
            G.stop = stop_after
            G.sfx = "" if li == 0 else "_l%d" % li
            if stop_after > 0:
                phase1(G, L, x_src)
            if stop_after >= 2:
                phase2(G, L)
            if stop_after >= 3:
                phase2l(G, L)
            if stop_after >= 4:
                phase3(G, L, x_src)
            if stop_after >= 5:
                phase4(G, L, last and final_norm, last)
    return nc


def load_vecT(G, st, name, rows_aps, n):
    nc, P = G.nc, G.P
    stg = Tl(st.enter_context(nc.sbuf_tensor(name + "_s" + G.sfx, [n, 128], F32)), name + "_s")
    out = Tl(st.enter_context(nc.sbuf_tensor(name + G.sfx, [128, n], F32)), name)
    P.dma([(stg[r:r + 1, c0:c0 + nc_], a) for (r, c0, nc_, a) in rows_aps], writes=[stg])
    ps = G.PS[6]
    P.op("pe", lambda e: e.transpose(ps[:, 0:n], stg[:, :], G.ident_f[0:n, 0:n]), reads=[stg, G.ident_f], writes=[ps])
    vop(P, "dve", out, [ps], lambda e: e.tensor_copy(out=out[:], in_=ps[:, 0:n]))
    return out


def rmsnorm_rows(G, x_tl, x_ap, ss_tl, junk_tl, rstd_tl):
    P = G.P
    act(P, junk_tl, junk_tl[:], x_ap, AF.Square, [x_tl], accum=ss_tl[:, 0:1], extra_w=[ss_tl])
    act(P, rstd_tl, rstd_tl[:, 0:1], ss_tl[:, 0:1], AF.Sqrt, [ss_tl], bias=G.eps_t[:, 0:1], scale=1.0 / 1024.0)
    vop(P, "dve", rstd_tl, [rstd_tl], lambda e: e.reciprocal(out=rstd_tl[:, 0:1], in_=rstd_tl[:, 0:1]))


def convert_weights(G, stage_ring, pieces, scale_ap_fn=None, scale_tl=None, maxc=2048):
    P = G.P
    flip = 0
    for (src, outs, kc) in pieces:
        np_, ncols = src.shape[0], src.shape[1]
        stg = stage_ring.next()
        P.dma([(stg[0:np_, 0:ncols], src)], writes=[stg])
        for (c0, n, dst_tl, dst_ap) in outs:
            if scale_ap_fn is not None:
                sc = scale_ap_fn(kc)
                rd = [stg, scale_tl]
                if flip % 2 == 0:
                    vop(P, "dve", dst_tl, rd, lambda e, dst_ap=dst_ap, stg=stg, c0=c0, n=n, sc=sc, np_=np_: e.tensor_scalar(
                        out=dst_ap, in0=stg[0:np_, c0:c0 + n], scalar1=sc, scalar2=None, op0=ALU.mult))
                else:
                    act(P, dst_tl, dst_ap, stg[0:np_, c0:c0 + n], AF.Identity, rd, scale=sc)
            else:
                if flip % 2 == 0:
                    vop(P, "dve", dst_tl, [stg], lambda e, dst_ap=dst_ap, stg=stg, c0=c0, n=n, np_=np_: e.tensor_copy(
                        out=dst_ap, in_=stg[0:np_, c0:c0 + n]))
                else:
                    act(P, dst_tl, dst_ap, stg[0:np_, c0:c0 + n], AF.Copy, [stg])
            flip += 1


def xn_transpose_block(G, st_tiles, x_src_tl, x_dram, tb, xnT_tl, ntok_sub, keep_x=None):
    P = G.P
    xr, xnr, ssr, jk = st_tiles
    for s in range(ntok_sub):
        t0 = tb + s * 128
        xt = xr.next() if keep_x is None else keep_x[s]
        P.dma([(xt[:], x_dram[t0:t0 + 128, :])], reads=[G.tok(x_src_tl.b.name, t0 // 128)], writes=[xt])
        ss = ssr.next()
        rs = ssr.next()
        rmsnorm_rows(G, xt, xt[:], ss, jk, rs)
        xn = xnr.next()
        vop(P, "dve", xn, [xt, rs], lambda e, xn=xn, xt=xt, rs=rs: e.tensor_scalar(
            out=xn[:], in0=xt[:], scalar1=rs[:, 0:1], scalar2=None, op0=ALU.mult))
        pst = G.PST

        def tfn(e, xn=xn):
            ins = None
            for kc in range(8):
                ins = e.transpose(pst[:, kc * 128:(kc + 1) * 128], xn[:, kc * 128:(kc + 1) * 128], G.ident_b[:])
            return ins
        P.op("pe", tfn, reads=[xn, G.ident_b], writes=[pst])
        act(P, xnT_tl, xnT_tl[:, :, s * 128:(s + 1) * 128], pst[:].rearrange("p (k t) -> p k t", k=8), AF.Copy, [pst])


def phase1(G, L, x_src):
    nc, P, I, R, PS, S, NB = G.nc, G.P, G.I, G.R, G.PS, G.S, G.NB
    with ExitStack() as st:
        def sb(name, shape, dt):
            return Tl(st.enter_context(nc.sbuf_tensor(name + G.sfx, list(shape), dt)), name)
        W1 = sb("p1_W1", [128, 8, 5120], BF16)
        stage = Ring([sb("p1_wst%d" % i, [128, 2432], F32) for i in range(2)])
        G.eps_t = sb("p1_eps", [128, 1], F32)
        vop(P, "dve", G.eps_t, [], lambda e: e.memset(G.eps_t[:], EPS))
        gmix = load_vecT(G, st, "p1_gmix", [(r, 0, 128, I["norm_mix_g"].ap()[L:L + 1, r * 128:(r + 1) * 128]) for r in range(8)], 8)
        qk = I["qk_norm_g"].ap()
        gqk = load_vecT(G, st, "p1_gqk", [(j, h * 64, 64, qk[L, j:j + 1, :]) for j in range(2) for h in range(2)], 2)
        w_in = I["w_in"].ap()
        pieces = []
        for kc in range(8):
            rows = slice(kc * 128, (kc + 1) * 128)
            for half in range(2):
                c_lo = half * 2432
                outs = []

                def add(src0, n, dst0):
                    a = max(src0, c_lo)
                    b = min(src0 + n, c_lo + 2432)
                    if b > a:
                        outs.append((a - c_lo, b - a, W1, W1[:, kc, dst0 + (a - src0): dst0 + (b - src0)]))
                add(O_AQ, 512, 0)
                for g in range(2):
                    add(O_AK + 64 * g, 64, 512 + g * 128)
                    add(O_AK + 64 * g, 64, 512 + g * 128 + 64)
                add(O_BX, 1024, 768)
                add(O_CQ, 512, 1792)
                for g in range(2):
                    add(O_CK + 64 * g, 64, 2304 + g * 128)
                    add(O_CK + 64 * g, 64, 2304 + g * 128 + 64)
                add(O_DQ, 1536, 2560)
                add(O_AV, 128, 4096)
                add(O_CV, 128, 4224)
                add(O_DV, 768, 4352)
                pieces.append((w_in[L, rows, c_lo:c_lo + 2432], outs, kc))
        convert_weights(G, stage, pieces, scale_ap_fn=lambda kc: gmix[:, kc:kc + 1], scale_tl=gmix, maxc=2432)

        xr = Ring([sb("p1_x%d" % i, [128, 1024], F32) for i in range(2)])
        xnr = Ring([sb("p1_xn%d" % i, [128, 1024], BF16) for i in range(2)])
        ssr = Ring([sb("p1_ss%d" % i, [128, 1], F32) for i in range(8)])
        jk = sb("p1_jk", [128, 1024], BF16)
        xnTr = Ring([sb("p1_xnT%d" % i, [128, 8, 512], BF16) for i in range(2)])
        ropeA = Ring([(sb("p1_cA%d" % i, [128, 512], F32), sb("p1_sA%d" % i, [128, 512], F32)) for i in range(1)])
        ropeS = Ring([(sb("p1_cS%d" % i, [128, 512], F32), sb("p1_sS%d" % i, [128, 512], F32)) for i in range(1)])
        qbr = Ring([sb("p1_qb%d" % i, [128, 512], BF16) for i in range(3)])
        sqr = Ring([sb("p1_sq%d" % i, [128, 512], BF16) for i in range(2)])
        rtr = Ring([sb("p1_rt%d" % i, [128, 512], F32) for i in range(2)])
        t1r = Ring([sb("p1_t1%d" % i, [128, 512], F32) for i in range(3)])
        t2r = Ring([sb("p1_t2%d" % i, [128, 512], F32) for i in range(3)])
        obr = Ring([sb("p1_ob%d" % i, [128, 512], BF16) for i in range(4)])
        ofr = Ring([sb("p1_of%d" % i, [128, 512], F32) for i in range(2)])
        vstr = Ring([sb("p1_vst%d" % i, [128, 16, 65], BF16) for i in range(2)])
        for v in vstr.items:
            vop(P, "dve", v, [], lambda e, v=v: e.memset(v[:], 1.0))
        pjr = Ring([PS[0], PS[1], PS[2]])
        swr = Ring([PS[3], PS[4]])
        vpr = Ring([PS[5], PS[6]])

        chunks = []
        for c in range(4):
            chunks.append(("A", R["QA"], c * 128))
        for g in range(2):
            chunks.append(("Ak", R["KA"], g * 128))
        for c in range(4):
            chunks.append(("B", R["BXT"], c * 128))
        for c in range(4):
            chunks.append(("B", R["BYT"], c * 128))
        for c in range(4):
            chunks.append(("S", R["QC"], c * 128))
        for g in range(2):
            chunks.append(("S", R["KC"], g * 128))
        for c in range(6):
            chunks.append(("S", R["QD"], c * 128))
        for c in range(6):
            chunks.append(("S", R["KD"], c * 128))

        x_dram = x_src.t.ap() if hasattr(x_src.t, "ap") else x_src.t
        for blk in range(NB if G.stop > 0.3 else 0):
            tb = blk * 512
            xnT = xnTr.next()
            xn_transpose_block(G, (xr, xnr, ssr, jk), x_src, x_dram, tb, xnT, 4)
            if G.stop < 0.7:
                P.dma([(R["XNT"].t.ap().rearrange("(k p) s -> p k s", p=128)[:, :, tb:tb + 512], xnT[:])],
                      reads=[xnT], writes=[G.tok("XNT", blk)])
                continue
            cA, sA = ropeA.next()
            cS, sS = ropeS.next()
            P.dma([(cA[:], I["c_cosA"].ap()[:, tb:tb + 512]), (sA[:], I["c_sinA"].ap()[:, tb:tb + 512])], writes=[cA, sA])
            P.dma([(cS[:], I["c_cosS"].ap()[:, tb:tb + 512]), (sS[:], I["c_sinS"].ap()[:, tb:tb + 512])], writes=[cS, sS])
            P.dma([(R["XNT"].t.ap().rearrange("(k p) s -> p k s", p=128)[:, :, tb:tb + 512], xnT[:])],
                  reads=[xnT], writes=[G.tok("XNT", blk)])
            for j, (kind, dst, r0) in enumerate(chunks):
                pj = pjr.next()
                mm(P, pj, pj[:], [(W1[:, kc, j * 128:(j + 1) * 128], xnT[:, kc, :]) for kc in range(8)], [W1, xnT])
                dst_ap = dst.t.ap()[r0:r0 + 128, tb:tb + 512]
                if kind == "B":
                    of = ofr.next()
                    act(P, of, of[:], pj[:], AF.Copy, [pj])
                    P.dma([(dst_ap, of[:])], reads=[of], writes=[G.tok(dst.b.name, blk)])
                    continue
                qb = qbr.next()
                if kind in ("A", "Ak"):
                    gcol = gqk[:, 0:1] if kind == "A" else gqk[:, 1:2]
                    sq = sqr.next()
                    act(P, sq, sq[:], pj[:], AF.Square, [pj])
                    sw = swr.next()
                    mm(P, sw, sw[:], [(G.bones_b[:], sq[:])], [G.bones_b, sq])
                    rt = rtr.next()
                    act(P, rt, rt[:], sw[:], AF.Sqrt, [sw], bias=G.eps_t[:, 0:1], scale=1.0 / 64.0)
                    vop(P, "dve", rt, [rt], lambda e, rt=rt: e.reciprocal(out=rt[:], in_=rt[:]))
                    qn = t1r.next()
                    vop(P, "dve", qn, [pj, rt, gqk], lambda e, qn=qn, pj=pj, rt=rt, gcol=gcol: e.scalar_tensor_tensor(
                        out=qn[:], in0=pj[:], scalar=gcol, in1=rt[:], op0=ALU.mult, op1=ALU.mult))
                    act(P, qb, qb[:], qn[:], AF.Copy, [qn])
                    src_tl, src_ap = qn, qn[:]
                    psw, cT, sT = G.pswA_b, cA, sA
                else:
                    act(P, qb, qb[:], pj[:], AF.Copy, [pj])
                    src_tl, src_ap = pj, pj[:]
                    psw, cT, sT = G.pswS_b, cS, sS
                sw = swr.next()
                mm(P, sw, sw[:], [(psw[:], qb[:])], [psw, qb])
                t1 = t1r.next()
                vop(P, "dve", t1, [src_tl, cT], lambda e, t1=t1, src_ap=src_ap, cT=cT: e.tensor_tensor(
                    out=t1[:], in0=src_ap, in1=cT[:], op=ALU.mult))
                t2 = t2r.next()
                vop(P, "dve", t2, [sw, sT], lambda e, t2=t2, sw=sw, sT=sT: e.tensor_tensor(
                    out=t2[:], in0=sw[:], in1=sT[:], op=ALU.mult))
                ob = obr.next()
                vop(P, "pool", ob, [t1, t2], lambda e, ob=ob, t1=t1, t2=t2: e.tensor_tensor(
                    out=ob[:], in0=t1[:], in1=t2[:], op=ALU.add))
                P.dma([(dst_ap, ob[:])], reads=[ob], writes=[G.tok(dst.b.name, blk)])
            for s in range(4):
                vst = vstr.next()
                for half in range(2):
                    vp = vpr.next()
                    mm(P, vp, vp[:], [(xnT[:, kc, s * 128:(s + 1) * 128], W1[:, kc, 4096 + half * 512: 4096 + (half + 1) * 512])
                                      for kc in range(8)], [W1, xnT])
                    act(P, vst, vst[:, half * 8:(half + 1) * 8, 0:64], vp[:].rearrange("p (h d) -> p h d", d=64), AF.Copy, [vp])
                n_ = (tb + s * 128) // 128
                vflat = vst[:].rearrange("p h d -> p (h d)")
                P.dma([(R["VA"].t.ap()[:, n_ * 130:(n_ + 1) * 130], vflat[:, 0:130]),
                       (R["VC"].t.ap()[:, n_ * 130:(n_ + 1) * 130], vflat[:, 130:260]),
                       (R["VD0"].t.ap()[:, n_ * 260:(n_ + 1) * 260], vflat[:, 260:520]),
                       (R["VD1"].t.ap()[:, n_ * 260:(n_ + 1) * 260], vflat[:, 520:780]),
                       (R["VD2"].t.ap()[:, n_ * 260:(n_ + 1) * 260], vflat[:, 780:1040])], reads=[vst],
                      writes=[G.tok("VS", n_)])
        P.emit()


def phase2(G, L):
    nc, P, I, R, PS, S, NB, NT = G.nc, G.P, G.I, G.R, G.PS, G.S, G.NB, G.NT
    with ExitStack() as st:
        def sb(name, shape, dt):
            return Tl(st.enter_context(nc.sbuf_tensor(name + G.sfx, list(shape), dt)), name)
        mstage = sb("p2_mst", [128, 1472], F32)
        masks = {}
        for k, wd in G.MASKW.items():
            m = sb("p2_mask" + k, [128, wd], BF16)
            for c0 in range(0, wd, 1472):
                c1 = min(wd, c0 + 1472)
                P.dma([(mstage[:, 0:c1 - c0], I["c_mask" + k].ap()[:, c0:c1])], writes=[mstage])
                vop(P, "dve", m, [mstage], lambda e, m=m, c0=c0, c1=c1: e.tensor_copy(out=m[:, c0:c1], in_=mstage[:, 0:c1 - c0]))
            masks[k] = m
        es = sb("p2_es", [128, 8], F32)
        P.dma([(es[:], dap(I["sink_logit"], L * 8, [[0, 128], [1, 8]]))], writes=[es])
        act(P, es, es[:], es[:], AF.Exp, [es])
        KA = sb("p2_KA", [128, 2, S], BF16)
        VA = sb("p2_VA", [128, NT, 2, 65], BF16)
        for g in range(2):
            for hlf in range(max(1, S // 2048)):
                w = min(S, 2048)
                P.dma([(KA[:, g, hlf * w:(hlf + 1) * w], R["KA"].t.ap()[g * 128:(g + 1) * 128, hlf * w:(hlf + 1) * w])],
                      writes=[KA])
        va3 = R["VA"].t.ap().rearrange("p (n h d) -> p n h d", h=2, d=65)
        vc3 = R["VC"].t.ap().rearrange("p (n h d) -> p n h d", h=2, d=65)
        vd3 = [R["VD%d" % gi].t.ap().rearrange("p (n h d) -> p n h d", h=4, d=65) for gi in range(3)]
        for n0 in range(0, NT, 16):
            n1 = min(NT, n0 + 16)
            P.dma([(VA[:, n0:n1, :, :], va3[:, n0:n1, :, :])], writes=[VA])

        QAr = Ring([sb("p2_QA%d" % i, [128, 4, 512], BF16) for i in range(2)])
        QCr = Ring([sb("p2_QC%d" % i, [128, 4, 512], BF16) for i in range(2)])
        QDr = Ring([sb("p2_QD%d" % i, [128, 6, 512], BF16) for i in range(2)])
        KCr = Ring([sb("p2_KC%d" % i, [128, 2, 768], BF16) for i in range(2)])
        VCr = Ring([sb("p2_VC%d" % i, [128, 6, 2, 65], BF16) for i in range(2)])
        KDg = [sb("p2_KD%d" % gi, [128, 2, w_], BF16) for gi, w_ in enumerate((768, 1024, 2560))]
        VDt = sb("p2_VD", [128, 34, 4, 65], BF16)
        PTr = Ring([sb("p2_PT%d" % i, [128, 512], BF16) for i in range(4)])
        Osr = Ring([sb("p2_Os%d" % i, [128, 512], F32) for i in range(2)])
        yAr = Ring([sb("p2_yA%d" % i, [64, 8, 512], BF16) for i in range(1)])
        yCr = Ring([sb("p2_yC%d" % i, [64, 8, 512], BF16) for i in range(1)])
        yDr = Ring([sb("p2_yD%d" % i, [64, 4, 512], BF16) for i in range(1)])
        str_ = Ring([PS[0], PS[1], PS[2]])
        accr = Ring([PS[3], PS[4]])
        bc = PS[5]

        def finish_head(acc, ydst, h, sink_h=None):
            Os = Osr.next()
            act(P, Os, Os[0:65, :], acc[0:65, :], AF.Copy, [acc])
            if sink_h is not None:
                vop(P, "dve", Os, [Os, es], lambda e, Os=Os, sink_h=sink_h: e.tensor_scalar(
                    out=Os[64:65, :], in0=Os[64:65, :], scalar1=es[64:65, sink_h:sink_h + 1], scalar2=None, op0=ALU.add))
            vop(P, "dve", Os, [Os], lambda e, Os=Os: e.reciprocal(out=Os[64:65, :], in_=Os[64:65, :]))
            mm(P, bc, bc[:, :], [(G.ones_f[64:65, 0:128], Os[64:65, :])], [G.ones_f, Os])
            vop(P, "dve", ydst, [Os, bc], lambda e, Os=Os, ydst=ydst, h=h: e.tensor_tensor(
                out=ydst[:, h, :], in0=Os[0:64, :], in1=bc[0:64, :], op=ALU.mult))

        def attn_step(acc, first, last, kT_tl, kT_ap, q_tl, q_ap, v_tl, v_ap, mask_ap=None, mask_tl=None):
            s_ps = str_.next()
            mm(P, s_ps, s_ps[:], [(kT_ap, q_ap)], [kT_tl, q_tl])
            pt = PTr.next()
            act(P, pt, pt[:], s_ps[:], AF.Exp, [s_ps], scale=0.125)
            if mask_ap is not None:
                vop(P, "dve", pt, [pt, mask_tl], lambda e, pt=pt, mask_ap=mask_ap: e.tensor_tensor(
                    out=pt[:], in0=pt[:], in1=mask_ap, op=ALU.mult))
            mm(P, acc, acc[0:65, :], [(v_ap, pt[:])], [v_tl, pt], start=first, stop=last)

        DG = [("D0", 64, 1, -1, 4, 0), ("D1", 256, 4, -2, 5, 6), ("D2", 1024, 16, -8, 11, 14)]
        qa3 = R["QA"].t.ap().rearrange("(c p) s -> p c s", p=128)
        qc3 = R["QC"].t.ap().rearrange("(c p) s -> p c s", p=128)
        qd3 = R["QD"].t.ap().rearrange("(c p) s -> p c s", p=128)
        kc3 = R["KC"].t.ap().rearrange("(g p) s -> p g s", p=128)
        kd3 = R["KD"].t.ap().rearrange("(c p) s -> p c s", p=128)
        for qb in range(NB):
            q0 = qb * 512
            QA_, QC_, QD_ = QAr.next(), QCr.next(), QDr.next()
            P.dma([(QA_[:], qa3[:, :, q0:q0 + 512])], writes=[QA_])
            P.dma([(QC_[:], qc3[:, :, q0:q0 + 512])], writes=[QC_])
            P.dma([(QD_[:], qd3[:, :, q0:q0 + 512])], writes=[QD_])
            yA = yAr.next()
            for h in range(8):
                g = h // 4
                pb = (h % 2) * 64
                acc = accr.next()
                for kb in range(NT):
                    attn_step(acc, kb == 0, kb == NT - 1, KA, KA[pb:pb + 64, g, kb * 128:(kb + 1) * 128], QA_, QA_[pb:pb + 64, h // 2, :],
                              VA, VA[:, kb, g, :])
                finish_head(acc, yA, h)
            P.dma([(R["YA"].t.ap().rearrange("d (h s) -> d h s", h=8)[:, :, q0:q0 + 512], yA[:])], reads=[yA],
                  writes=[G.tok("YA", qb)])
            kb_lo = max(0, qb * 4 - 1)
            kb_hi = min(NT - 1, qb * 4 + 4)
            nk = kb_hi - kb_lo + 1
            KC_, VC_ = KCr.next(), VCr.next()
            P.dma([(KC_[:, :, 0:nk * 128], kc3[:, :, kb_lo * 128:(kb_hi + 1) * 128])], writes=[KC_])
            P.dma([(VC_[:, 0:nk, :, :], vc3[:, kb_lo:kb_hi + 1, :, :])], writes=[VC_])
            yC = yCr.next()
            mC = masks["C"]
            for h in range(8):
                g = h // 4
                pb = (h % 2) * 64
                acc = accr.next()
                first = True
                for kb in range(kb_lo, kb_hi + 1):
                    r = kb - qb * 4
                    i = kb - kb_lo
                    attn_step(acc, first, kb == kb_hi, KC_, KC_[pb:pb + 64, g, i * 128:(i + 1) * 128], QC_, QC_[pb:pb + 64, h // 2, :],
                              VC_, VC_[:, i, g, :], mask_ap=mC[:, (4 - r) * 128:(4 - r) * 128 + 512], mask_tl=mC)
                    first = False
                finish_head(acc, yC, h, sink_h=h)
            P.dma([(R["YC"].t.ap().rearrange("d (h s) -> d h s", h=8)[:, :, q0:q0 + 512], yC[:])], reads=[yC],
                  writes=[G.tok("YC", qb)])
            rng = []
            for gi, (mk, W, dil, rmin, rmax, slot0) in enumerate(DG):
                lo = max(0, qb * 4 + rmin)
                hi = min(NT - 1, qb * 4 + rmax)
                rng.append((lo, hi))
                n = hi - lo + 1
                P.dma([(KDg[gi][:, :, 0:n * 128], kd3[:, 2 * gi:2 * gi + 2, lo * 128:(hi + 1) * 128])], writes=[KDg[gi]])
                P.dma([(VDt[:, slot0:slot0 + n, :, :], vd3[gi][:, lo:hi + 1, :, :])], writes=[VDt])
            yD = yDr.next()
            for h in range(4):
                acc = accr.next()
                first = True
                for gi, (mk, W, dil, rmin, rmax, slot0) in enumerate(DG):
                    gh = gi * 4 + h
                    c = gh // 2
                    pb = (gh % 2) * 64
                    lo, hi = rng[gi]
                    mt = masks[mk]
                    for kb in range(lo, hi + 1):
                        r = kb - qb * 4
                        i = kb - lo
                        attn_step(acc, first, (gi == 2 and kb == hi), KDg[gi], KDg[gi][pb:pb + 64, c - 2 * gi, i * 128:(i + 1) * 128], QD_, QD_[pb:pb + 64, c, :],
                                  VDt, VDt[:, slot0 + i, h, :], mask_ap=mt[:, (rmax - r) * 128:(rmax - r) * 128 + 512], mask_tl=mt)
                        first = False
                finish_head(acc, yD, h)
            P.dma([(R["YD"].t.ap().rearrange("d (h s) -> d h s", h=4)[:, :, q0:q0 + 512], yD[:])], reads=[yD],
                  writes=[G.tok("YD", qb)])
        P.emit()


def phase2l(G, L):
    nc, P, I, R, PS, S = G.nc, G.P, G.I, G.R, G.PS, G.S
    TS = min(S, 2048)
    NSEG = S // TS
    with ExitStack() as st:
        def sb(name, shape, dt):
            return Tl(st.enter_context(nc.sbuf_tensor(name + G.sfx, list(shape), dt)), name)
        cw = I["conv_w"].ap()
        cwT = load_vecT(G, st, "pl_cw", [(j * 4 + c, 0, 128, cw[L, j:j + 1, c * 128:(c + 1) * 128]) for j in range(4) for c in range(4)], 16)
        cbT = load_vecT(G, st, "pl_cb", [(c, 0, 128, I["conv_b"].ap()[L:L + 1, c * 128:(c + 1) * 128]) for c in range(4)], 4)
        gb = I["lru_gate_b"].ap()
        gbT = load_vecT(G, st, "pl_gb", [((d * 2 + g) * 4 + c, 0, 128, gb[L, d, g:g + 1, c * 128:(c + 1) * 128])
                                         for d in range(2) for g in range(2) for c in range(4)], 16)
        lam = I["lru_lambda"].ap()
        lamT = load_vecT(G, st, "pl_lam", [(d * 4 + c, 0, 128, lam[L, d:d + 1, c * 128:(c + 1) * 128]) for d in range(2) for c in range(4)], 8)
        act(P, lamT, lamT[:], lamT[:], AF.Exp, [lamT], scale=-1.0)
        act(P, lamT, lamT[:], lamT[:], AF.Ln, [lamT], bias=G.ones_f[:, 0:1], scale=1.0)
        vop(P, "dve", lamT, [lamT], lambda e: e.tensor_scalar(out=lamT[:], in0=lamT[:], scalar1=-8.0, scalar2=None, op0=ALU.mult))
        nsp8 = lamT
        gws = sb("pl_gws", [128, 16, 128], F32)
        GW = sb("pl_GW", [128, 16, 128], BF16)
        vop(P, "dve", gws, [], lambda e: e.memset(gws[:], 0.0))
        gw = I["lru_gate_w"]
        prs = []
        for d in range(2):
            for g in range(2):
                for par in range(2):
                    base = (((L * 2 + d) * 2 + g) * 8 + par) * 4096
                    src = dap(gw, base, [[64, 64], [2 * 4096, 4], [1, 64]])
                    idx0 = (d * 2 + g) * 4
                    prs.append((gws[par * 64:(par + 1) * 64, idx0:idx0 + 4, par * 64:(par + 1) * 64], src))
        P.dma(prs, reads=[gws], writes=[gws])
        vop(P, "dve", GW, [gws], lambda e: e.tensor_copy(out=GW[:], in_=gws[:]))

        bxs_r = Ring([sb("pl_bxs%d" % i, [128, TS + 3], F32) for i in range(2)])
        xc_r = Ring([sb("pl_xc%d" % i, [128, TS], F32) for i in range(2)])
        xcb_r = Ring([sb("pl_xcb%d" % i, [128, TS], BF16) for i in range(2)])
        rr_r = Ring([sb("pl_r%d" % i, [128, TS], F32) for i in range(2)])
        ii_r = Ring([sb("pl_i%d" % i, [128, TS], F32) for i in range(2)])
        aa_r = Ring([sb("pl_a%d" % i, [128, TS], F32) for i in range(2)])
        uu_r = Ring([sb("pl_u%d" % i, [128, TS], F32) for i in range(2)])
        hh_r = Ring([sb("pl_h%d" % i, [128, TS], F32) for i in range(2)])
        hf_r = Ring([sb("pl_hf%d" % i, [128, TS], F32) for i in range(2)])
        by_r = Ring([sb("pl_by%d" % i, [128, TS], F32) for i in range(2)])
        yb_r = Ring([sb("pl_yb%d" % i, [128, TS], BF16) for i in range(2)])
        carry = sb("pl_carry", [128, 1], F32)
        psr = Ring([PS[0], PS[1], PS[2], PS[3]])

        def seg(d, c, sg):
            t0 = sg * TS
            bxs = bxs_r.next()
            lo = max(0, t0 - 2)
            hi = min(S, t0 + TS + 1)
            if lo > t0 - 2 or hi < t0 + TS + 1:
                vop(P, "dve", bxs, [], lambda e, bxs=bxs: e.memset(bxs[:], 0.0))
            P.dma([(bxs[:, lo - (t0 - 2):hi - (t0 - 2)], R["BXT"].t.ap()[c * 128:(c + 1) * 128, lo:hi])], writes=[bxs])
            xc = xc_r.next()
            act(P, xc, xc[:], bxs[:, 0:TS], AF.Identity, [bxs, cwT, cbT], bias=cbT[:, c:c + 1], scale=cwT[:, c:c + 1])
            for j in range(1, 4):
                vop(P, "dve", xc, [xc, bxs, cwT], lambda e, xc=xc, bxs=bxs, j=j: e.scalar_tensor_tensor(
                    out=xc[:], in0=bxs[:, j:j + TS], scalar=cwT[:, j * 4 + c:j * 4 + c + 1], in1=xc[:], op0=ALU.mult, op1=ALU.add))
            xcb = xcb_r.next()
            act(P, xcb, xcb[:], xc[:], AF.Copy, [xc])
            rr, ii = rr_r.next(), ii_r.next()
            for g, dst in ((0, rr), (1, ii)):
                idx = (d * 2 + g) * 4 + c
                for ts_ in range(TS // 512):
                    ps = psr.next()
                    mm(P, ps, ps[:], [(GW[:, idx, :], xcb[:, ts_ * 512:(ts_ + 1) * 512])], [GW, xcb])
                    act(P, dst, dst[:, ts_ * 512:(ts_ + 1) * 512], ps[:], AF.Sigmoid, [ps, gbT], bias=gbT[:, idx:idx + 1])
            aa = aa_r.next()
            act(P, aa, aa[:], rr[:], AF.Exp, [rr, nsp8], scale=nsp8[:, d * 4 + c:d * 4 + c + 1])
            act(P, rr, rr[:], aa[:], AF.Square, [aa])
            act(P, rr, rr[:], rr[:], AF.Sqrt, [rr], bias=G.ones_f[:, 0:1], scale=-1.0)
            uu = uu_r.next()
            vop(P, "dve", uu, [ii, xc], lambda e, uu=uu, ii=ii, xc=xc: e.tensor_tensor(out=uu[:], in0=ii[:], in1=xc[:], op=ALU.mult))
            vop(P, "dve", uu, [uu, rr], lambda e, uu=uu, rr=rr: e.tensor_tensor(out=uu[:], in0=uu[:], in1=rr[:], op=ALU.mult))
            return aa, uu

        for c in range(4):
            vop(P, "dve", carry, [], lambda e: e.memset(carry[:], 0.0))
            for sg in range(NSEG):
                aa, uu = seg(0, c, sg)
                hh = hh_r.next()
                vop(P, "dve", hh, [aa, uu, carry], lambda e, hh=hh, aa=aa, uu=uu: e.tensor_tensor_scan(
                    out=hh[:], data0=aa[:], data1=uu[:], initial=carry[:, 0:1], op0=ALU.mult, op1=ALU.add))
                vop(P, "dve", carry, [hh], lambda e, hh=hh: e.tensor_copy(out=carry[:], in_=hh[:, TS - 1:TS]))
                P.dma([(R["HF"].t.ap()[c * 128:(c + 1) * 128, sg * TS:(sg + 1) * TS], hh[:])], reads=[hh], writes=[G.tok("HF", (c, sg))])
            vop(P, "dve", carry, [], lambda e: e.memset(carry[:], 0.0))
            for sg in range(NSEG - 1, -1, -1):
                aa, uu = seg(1, c, sg)
                hh = hh_r.next()
                vop(P, "dve", hh, [aa, uu, carry], lambda e, hh=hh, aa=aa, uu=uu: e.tensor_tensor_scan(
                    out=hh[:, ::-1], data0=aa[:, ::-1], data1=uu[:, ::-1], initial=carry[:, 0:1], op0=ALU.mult, op1=ALU.add))
                vop(P, "dve", carry, [hh], lambda e, hh=hh: e.tensor_copy(out=carry[:], in_=hh[:, 0:1]))
                hf, by = hf_r.next(), by_r.next()
                P.dma([(hf[:], R["HF"].t.ap()[c * 128:(c + 1) * 128, sg * TS:(sg + 1) * TS])], reads=[G.tok("HF", (c, sg))], writes=[hf])
                P.dma([(by[:], R["BYT"].t.ap()[c * 128:(c + 1) * 128, sg * TS:(sg + 1) * TS])], writes=[by])
                vop(P, "pool", hf, [hf, hh], lambda e, hf=hf, hh=hh: e.tensor_tensor(out=hf[:], in0=hf[:], in1=hh[:], op=ALU.add))
                gg = ii_r.next()
                act(P, gg, gg[:], by[:], AF.Square, [by])
                vop(P, "dve", gg, [gg], lambda e, gg=gg: e.tensor_scalar(out=gg[:], in0=gg[:], scalar1=0.044715, scalar2=1.0,
                                                                         op0=ALU.mult, op1=ALU.add))
                vop(P, "dve", gg, [gg, by], lambda e, gg=gg, by=by: e.tensor_tensor(out=gg[:], in0=gg[:], in1=by[:], op=ALU.mult))
                act(P, gg, gg[:], gg[:], AF.Sigmoid, [gg], scale=1.5957691216057308)
                vop(P, "dve", gg, [gg, by], lambda e, gg=gg, by=by: e.tensor_tensor(out=gg[:], in0=gg[:], in1=by[:], op=ALU.mult))
                yb = yb_r.next()
                vop(P, "dve", yb, [gg, hf], lambda e, yb=yb, gg=gg, hf=hf: e.tensor_tensor(out=yb[:], in0=gg[:], in1=hf[:], op=ALU.mult))
                P.dma([(R["YBT"].t.ap()[c * 128:(c + 1) * 128, sg * TS:(sg + 1) * TS], yb[:])], reads=[yb], writes=[G.tok("YBT", (c, sg))])
        P.emit()


def phase3(G, L, x_src):
    nc, P, I, R, PS, S, NB = G.nc, G.P, G.I, G.R, G.PS, G.S, G.NB
    with ExitStack() as st:
        def sb(name, shape, dt):
            return Tl(st.enter_context(nc.sbuf_tensor(name + G.sfx, list(shape), dt)), name)
        WG = sb("p3_WG", [128, 8, 4096], BF16)
        WPa = sb("p3_WPa", [64, 8, 1024], BF16)
        WPc = sb("p3_WPc", [64, 8, 1024], BF16)
        WPd = sb("p3_WPd", [64, 4, 1024], BF16)
        WPb = sb("p3_WPb", [128, 4, 1024], BF16)
        WO = sb("p3_WO", [128, 8, 1024], BF16)
        stage = Ring([sb("p3_wst%d" % i, [128, 1024], F32) for i in range(2)])
        gmix = load_vecT(G, st, "p3_gmix", [(r, 0, 128, I["norm_mix_g"].ap()[L:L + 1, r * 128:(r + 1) * 128]) for r in range(8)], 8)
        gbi = I["gate_bias"].ap()
        gbT = load_vecT(G, st, "p3_gb", [(i * 8 + f, 0, 128, gbi[L, i:i + 1, f * 128:(f + 1) * 128]) for i in range(4) for f in range(8)], 32)
        w_in = I["w_in"].ap()
        pieces = []
        for kc in range(8):
            for hf in range(4):
                pieces.append((w_in[L, kc * 128:(kc + 1) * 128, O_GL + hf * 1024:O_GL + (hf + 1) * 1024],
                               [(0, 1024, WG, WG[:, kc, hf * 1024:(hf + 1) * 1024])], kc))
        convert_weights(G, stage, pieces, scale_ap_fn=lambda kc: gmix[:, kc:kc + 1], scale_tl=gmix)
        pieces = []
        for nm, Wt, nh in (("w_proj_a", WPa, 8), ("w_proj_c", WPc, 8), ("w_proj_d", WPd, 4)):
            w = I[nm].ap()
            for h0 in range(0, nh, 2):
                pieces.append((w[L, h0 * 64:h0 * 64 + 64, :], [(0, 1024, Wt, Wt[:, h0, :])], 0))
                pieces.append((w[L, h0 * 64 + 64:h0 * 64 + 128, :], [(0, 1024, Wt, Wt[:, h0 + 1, :])], 0))
        for c in range(4):
            pieces.append((I["w_proj_b"].ap()[L, c * 128:(c + 1) * 128, :], [(0, 1024, WPb, WPb[:, c, :])], 0))
        for c in range(8):
            pieces.append((I["w_out"].ap()[L, c * 128:(c + 1) * 128, :], [(0, 1024, WO, WO[:, c, :])], 0))
        convert_weights(G, stage, pieces)

        xnTr = Ring([sb("p3_xnT%d" % i, [128, 8, 512], BF16) for i in range(1)])
        yAt = sb("p3_yA", [64, 8, 512], BF16)
        yCt = sb("p3_yC", [64, 8, 512], BF16)
        yDt = sb("p3_yD", [64, 4, 512], BF16)
        yBt = sb("p3_yB", [128, 4, 512], BF16)
        gtr = Ring([sb("p3_g%d" % i, [128, 512], BF16) for i in range(6)])
        mr = Ring([sb("p3_m%d" % i, [128, 512], F32) for i in range(4)])
        mT = sb("p3_mT", [128, 8, 512], BF16)
        xr = Ring([sb("p3_x%d" % i, [128, 1024], F32) for i in range(2)])
        glr = Ring([PS[0], PS[1], PS[2], PS[3]])
        ppr = Ring([PS[4], PS[5]])
        opr = Ring([PS[6], PS[5]])
        x_dram = x_src.t.ap() if hasattr(x_src.t, "ap") else x_src.t
        xnt3 = R["XNT"].t.ap().rearrange("(k p) s -> p k s", p=128)
        ybt3 = R["YBT"].t.ap().rearrange("(c p) s -> p c s", p=128)
        for blk in range(NB):
            tb = blk * 512
            xnT = xnTr.next()
            P.dma([(xnT[:], xnt3[:, :, tb:tb + 512])], writes=[xnT])
            P.dma([(yAt[:], R["YA"].t.ap().rearrange("d (h s) -> d h s", h=8)[:, :, tb:tb + 512])], writes=[yAt])
            P.dma([(yCt[:], R["YC"].t.ap().rearrange("d (h s) -> d h s", h=8)[:, :, tb:tb + 512])], writes=[yCt])
            P.dma([(yDt[:], R["YD"].t.ap().rearrange("d (h s) -> d h s", h=4)[:, :, tb:tb + 512])], writes=[yDt])
            P.dma([(yBt[:], ybt3[:, :, tb:tb + 512])], writes=[yBt])
            for f in range(8):
                fs = slice(f * 128, (f + 1) * 128)
                gts = []
                for i in range(4):
                    ps = glr.next()
                    mm(P, ps, ps[:], [(WG[:, kc, i * 1024 + f * 128:i * 1024 + (f + 1) * 128], xnT[:, kc, :]) for kc in range(8)], [WG, xnT])
                    gt = gtr.next()
                    act(P, gt, gt[:], ps[:], AF.Sigmoid, [ps, gbT], bias=gbT[:, i * 8 + f:i * 8 + f + 1])
                    gts.append(gt)
                prods = []
                for i, (Wt, yt, nh) in enumerate(((WPa, yAt, 8), (WPb, yBt, 4), (WPc, yCt, 8), (WPd, yDt, 4))):
                    pp = ppr.next()
                    mm(P, pp, pp[:], [(Wt[:, h, fs], yt[:, h, :]) for h in range(nh)], [Wt, yt])
                    m = mr.next()
                    vop(P, "dve", m, [pp, gts[i]], lambda e, m=m, pp=pp, gt=gts[i]: e.tensor_tensor(out=m[:], in0=pp[:], in1=gt[:], op=ALU.mult))
                    prods.append(m)
                vop(P, "pool", prods[0], [prods[0], prods[1]], lambda e, a=prods[0], b=prods[1]: e.tensor_tensor(out=a[:], in0=a[:], in1=b[:], op=ALU.add))
                vop(P, "pool", prods[2], [prods[2], prods[3]], lambda e, a=prods[2], b=prods[3]: e.tensor_tensor(out=a[:], in0=a[:], in1=b[:], op=ALU.add))
                vop(P, "pool", mT, [prods[0], prods[2]], lambda e, a=prods[0], b=prods[2], f=f: e.tensor_tensor(out=mT[:, f, :], in0=a[:], in1=b[:], op=ALU.add))
            for s in range(4):
                t0 = tb + s * 128
                xt = xr.next()
                P.dma([(xt[:], x_dram[t0:t0 + 128, :])], reads=[G.tok(x_src.b.name, t0 // 128)], writes=[xt])
                for hf in range(2):
                    op_ = opr.next()
                    mm(P, op_, op_[:], [(mT[:, f, s * 128:(s + 1) * 128], WO[:, f, hf * 512:(hf + 1) * 512]) for f in range(8)], [mT, WO])
                    vop(P, "dve", xt, [xt, op_], lambda e, xt=xt, op_=op_, hf=hf: e.tensor_tensor(
                        out=xt[:, hf * 512:(hf + 1) * 512], in0=xt[:, hf * 512:(hf + 1) * 512], in1=op_[:], op=ALU.add))
                P.dma([(R["X1"].t.ap()[t0:t0 + 128, :], xt[:])], reads=[xt], writes=[G.tok("X1", t0 // 128)])
        P.emit()


def phase4(G, L, final_norm, to_out):
    nc, P, I, R, PS, S = G.nc, G.P, G.I, G.R, G.PS, G.S
    TBM = 256
    with ExitStack() as st:
        def sb(name, shape, dt):
            return Tl(st.enter_context(nc.sbuf_tensor(name + G.sfx, list(shape), dt)), name)
        Wm1 = sb("p4_W1", [128, 8, 4096], BF16)
        Wm2 = sb("p4_W2", [128, 32, 1024], BF16)
        stage = Ring([sb("p4_wst%d" % i, [128, 1024], F32) for i in range(2)])
        G.eps_t = sb("p4_eps", [128, 1], F32)
        vop(P, "dve", G.eps_t, [], lambda e: e.memset(G.eps_t[:], EPS))
        gm = load_vecT(G, st, "p4_gm", [(r, 0, 128, I["norm_mlp_g"].ap()[L:L + 1, r * 128:(r + 1) * 128]) for r in range(8)], 8)
        pieces = []
        for kc in range(8):
            for hf in range(4):
                pieces.append((I["w_mlp1"].ap()[L, kc * 128:(kc + 1) * 128, hf * 1024:(hf + 1) * 1024],
                               [(0, 1024, Wm1, Wm1[:, kc, hf * 1024:(hf + 1) * 1024])], kc))
        convert_weights(G, stage, pieces, scale_ap_fn=lambda kc: gm[:, kc:kc + 1], scale_tl=gm)
        pieces = []
        for hc in range(32):
            pieces.append((I["w_mlp2"].ap()[L, hc * 128:(hc + 1) * 128, :], [(0, 1024, Wm2, Wm2[:, hc, :])], 0))
        convert_weights(G, stage, pieces)
        gfin = None
        if final_norm:
            gfin = sb("p4_gfin", [128, 1024], F32)
            P.dma([(gfin[:], I["norm_final_g"].ap().partition_broadcast(128))], writes=[gfin])
            finT = sb("p4_fin", [128, 1], F32)
            P.dma([(finT[:], I["fin"].ap())], writes=[finT])
            ynr = Ring([sb("p4_yn%d" % i, [128, 1024], F32) for i in range(2)])

        nsub = TBM // 128
        xkeep = [sb("p4_xk%d" % i, [128, 1024], F32) for i in range(nsub)]
        xnr = Ring([sb("p4_xn%d" % i, [128, 1024], BF16) for i in range(2)])
        ssr = Ring([sb("p4_ss%d" % i, [128, 1], F32) for i in range(8)])
        jk = sb("p4_jk", [128, 1024], BF16)
        xnT = sb("p4_xnT", [128, 8, TBM], BF16)
        hT = sb("p4_hT", [128, 32, TBM], BF16)
        rlr = Ring([sb("p4_rl%d" % i, [128, TBM], F32) for i in range(3)])
        outr = Ring([sb("p4_o%d" % i, [128, 1024], F32) for i in range(2)])
        hpr = Ring([PS[0], PS[1], PS[2]])
        opr = Ring([PS[3], PS[4], PS[5], PS[6]])
        X1 = R["X1"]
        dst_dram = G.y.ap() if to_out else R["X2"].t.ap()
        dst_name = "Y" if to_out else "X2"
        for blk in range(S // TBM):
            tb = blk * TBM
            xn_transpose_block(G, (None, xnr, ssr, jk), X1, X1.t.ap(), tb, xnT, nsub, keep_x=xkeep)
            for hc in range(32):
                hp = hpr.next()
                mm(P, hp, hp[:, 0:TBM], [(Wm1[:, kc, hc * 128:(hc + 1) * 128], xnT[:, kc, :]) for kc in range(8)], [Wm1, xnT])
                rl = rlr.next()
                act(P, rl, rl[:], hp[:, 0:TBM], AF.Relu, [hp])
                vop(P, "dve" if hc % 2 == 0 else "pool", hT, [rl], lambda e, rl=rl, hc=hc: e.tensor_tensor(
                    out=hT[:, hc, :], in0=rl[:], in1=rl[:], op=ALU.mult))
            for s in range(nsub):
                t0 = tb + s * 128
                ot = outr.next()
                for hf in range(2):
                    op_ = opr.next()
                    mm(P, op_, op_[:], [(hT[:, hc, s * 128:(s + 1) * 128], Wm2[:, hc, hf * 512:(hf + 1) * 512]) for hc in range(32)], [hT, Wm2])
                    vop(P, "dve", ot, [xkeep[s], op_], lambda e, ot=ot, xk=xkeep[s], op_=op_, hf=hf: e.tensor_tensor(
                        out=ot[:, hf * 512:(hf + 1) * 512], in0=xk[:, hf * 512:(hf + 1) * 512], in1=op_[:], op=ALU.add))
                if final_norm:
                    ss, rs = ssr.next(), ssr.next()
                    rmsnorm_rows(G, ot, ot[:], ss, jk, rs)
                    yn = ynr.next()
                    vop(P, "dve", yn, [ot, rs, gfin], lambda e, yn=yn, ot=ot, rs=rs: e.scalar_tensor_tensor(
                        out=yn[:], in0=ot[:], scalar=rs[:, 0:1], in1=gfin[:], op0=ALU.mult, op1=ALU.mult))
                    vop(P, "pool", yn, [yn, ot], lambda e, yn=yn, ot=ot: e.tensor_tensor(out=yn[:], in0=yn[:], in1=ot[:], op=ALU.subtract))
                    vop(P, "dve", ot, [yn, ot, finT], lambda e, yn=yn, ot=ot: e.scalar_tensor_tensor(
                        out=ot[:], in0=yn[:], scalar=finT[:, 0:1], in1=ot[:], op0=ALU.mult, op1=ALU.add))
                P.dma([(dst_dram[t0:t0 + 128, :], ot[:])], reads=[ot], writes=[G.tok(dst_name, t0 // 128)])
        P.emit()


def make_consts(S):
    c = {}
    c["c_ident"] = np.eye(128, dtype=np.float32)
    bo = np.zeros((128, 128), np.float32)
    bo[:64, :64] = 1.0
    bo[64:, 64:] = 1.0
    c["c_bones"] = bo
    p = np.arange(128)
    d = p % 64
    partA = np.where((d % 32) < 16, p + 16, p - 16)
    partS = np.where(d < 32, p + 32, p - 32)
    pa = np.zeros((128, 128), np.float32)
    pa[partA, p] = 1.0
    ps = np.zeros((128, 128), np.float32)
    ps[partS, p] = 1.0
    c["c_pswA"], c["c_pswS"] = pa, ps
    t = np.arange(S)
    invS = (THETA ** (-(np.arange(0, 64, 2, dtype=np.float32)) / 64.0)).astype(np.float32)
    angS = t[None, :].astype(np.float32) * invS[d % 32][:, None]
    sgnS = np.where(d < 32, -1.0, 1.0).astype(np.float32)[:, None]
    c["c_cosS"] = np.cos(angS).astype(np.float32)
    c["c_sinS"] = (np.sin(angS) * sgnS).astype(np.float32)
    invA = (THETA ** (-(np.arange(0, 32, 2, dtype=np.float32)) / 32.0)).astype(np.float32)
    row = (t // GRID_W).astype(np.float32)
    col = (t % GRID_W).astype(np.float32)
    pos = np.where((d < 32)[:, None], row[None, :], col[None, :])
    angA = pos * invA[(d % 32) % 16][:, None]
    sgnA = np.where((d % 32) < 16, -1.0, 1.0).astype(np.float32)[:, None]
    c["c_cosA"] = np.cos(angA).astype(np.float32)
    c["c_sinA"] = (np.sin(angA) * sgnA).astype(np.float32)
    kk = np.arange(128)[:, None]

    def mk(width, W, dil, rmax):
        u = np.arange(width)[None, :]
        delta = kk - u + rmax * 128
        return ((np.abs(delta) <= W) & (delta % dil == 0)).astype(np.float32)
    c["c_maskC"] = mk(1152, 128, 1, 4)
    c["c_maskD0"] = mk(1152, 64, 1, 4)
    c["c_maskD1"] = mk(1408, 256, 4, 5)
    c["c_maskD2"] = mk(2944, 1024, 16, 11)
    return c


_PROG_CACHE = {}
WEIGHT_NAMES = ["norm_mix_g", "w_in", "gate_bias", "qk_norm_g", "conv_w", "conv_b", "lru_gate_w", "lru_gate_b", "lru_lambda",
                "sink_logit", "w_proj_a", "w_proj_b", "w_proj_c", "w_proj_d", "w_out", "norm_mlp_g", "w_mlp1", "w_mlp2"]


def kernel(**inputs):
    x = np.ascontiguousarray(np.asarray(inputs["x"], dtype=np.float32))
    B, S, _ = x.shape
    depth = inputs["w_in"].shape[0]
    key = (S, 1)
    if key not in _PROG_CACHE:
        _PROG_CACHE[key] = build(S, [0], 1, final_norm=True)
    nc = _PROG_CACHE[key]
    consts = make_consts(S)
    gfin = np.ascontiguousarray(np.asarray(inputs["norm_final_g"], dtype=np.float32))
    cur = [x[b] for b in range(B)]
    for l in range(depth):
        base = {k: np.ascontiguousarray(np.asarray(inputs[k], dtype=np.float32)[l:l + 1]) for k in WEIGHT_NAMES}
        base["norm_final_g"] = gfin
        base["fin"] = np.full((128, 1), 1.0 if l == depth - 1 else 0.0, dtype=np.float32)
        base.update(consts)
        in_maps = []
        for b in range(B):
            m = dict(base)
            m["x"] = np.ascontiguousarray(cur[b], dtype=np.float32)
            in_maps.append(m)
        res = run_bass_kernel_spmd(nc, in_maps, core_ids=list(range(B)))
        cur = [np.asarray(r["y"], dtype=np.float32) for r in res.results]
    return np.stack(cur, axis=0)
```
